# Optimizing a Trainium2 kernel written in Bass

```python
import jax, jax.numpy as jnp
from jax import lax
import numpy as np

D_MODEL = 4096
BATCH = 4
SEQ = 2048
DEPTH = 1

N_META = 16
MIX_WIDTH = D_MODEL
CONV_CH = MIX_WIDTH // 2
POOL_CH = MIX_WIDTH - CONV_CH
CONV_HEADS = 16
CONV_K = 3
POOL_WINDOWS = (2, 4, 8, 16)
N_POOL_GROUPS = len(POOL_WINDOWS)
POOL_GROUP = POOL_CH // N_POOL_GROUPS
IN_COLS = 3 * CONV_CH + POOL_CH
D_FF = 256 * ((8 * D_MODEL // 3 + 255) // 256)
LN_EPS = 1e-5
ALPHA = (2.0 * DEPTH) ** 0.25
BETA = (8.0 * DEPTH) ** -0.25

kernel_name = "hybrid_conv_pool_macaron_deepnorm"


def layer_norm(x, g, b):
    xf = x.astype(jnp.float32)
    mu = jnp.mean(xf, axis=-1, keepdims=True)
    xc = xf - mu
    var = jnp.mean(jnp.square(xc), axis=-1, keepdims=True)
    y = xc * lax.rsqrt(var + LN_EPS) * g.astype(jnp.float32) + b.astype(jnp.float32)
    return y.astype(x.dtype)


def swiglu_ffn(x, w_gu, w_down):
    gu = jnp.einsum('bld,df->blf', x, w_gu)
    gate, up = jnp.split(gu, 2, axis=-1)
    return jnp.einsum('blf,fd->bld', jax.nn.silu(gate) * up, w_down)


def causal_short_conv(z, w):
    L = z.shape[1]
    zp = jnp.pad(z, ((0, 0), (CONV_K - 1, 0), (0, 0)))
    y = zp[:, 0:L] * w[0]
    for k in range(1, CONV_K):
        y = y + zp[:, k:k + L] * w[k]
    return y


def causal_window_mean(z, window):
    L = z.shape[1]
    cs = jnp.cumsum(z, axis=1)
    prev = jnp.pad(cs, ((0, 0), (window, 0), (0, 0)))[:, :L]
    count = jnp.minimum(jnp.arange(1, L + 1), window).astype(jnp.float32)
    return (cs - prev) / count[None, :, None]


def pooling_mixer(z, pool_w, pool_scale):
    b, L, _ = z.shape
    zg = z.reshape(b, L, N_POOL_GROUPS, POOL_GROUP).astype(jnp.float32)
    pooled = jnp.stack([causal_window_mean(zg[:, :, g], POOL_WINDOWS[g])
                        for g in range(N_POOL_GROUPS)], axis=2)
    d = (pooled - zg).astype(z.dtype)
    y = jnp.einsum('blgc,gcd->blgd', d, pool_w).reshape(b, L, POOL_CH)
    return y * pool_scale


def hybrid_mixer(h, w_in, conv_w, pool_w, pool_scale, w_out):
    u = jnp.einsum('bld,dc->blc', h, w_in)
    gate_b = u[..., 0:CONV_CH]
    gate_c = u[..., CONV_CH:2 * CONV_CH]
    x_in = u[..., 2 * CONV_CH:3 * CONV_CH]
    z_pool = u[..., 3 * CONV_CH:]
    y_conv = gate_b * causal_short_conv(gate_c * x_in, conv_w)
    y_pool = pooling_mixer(z_pool, pool_w, pool_scale)
    y = jnp.concatenate([y_conv, y_pool], axis=-1)
    return jnp.einsum('blc,cd->bld', y, w_out)


def setup_inputs(seed: int = 0) -> dict:
    key = jax.random.key(seed)
    ks = jax.random.split(key, 20)
    f32 = jnp.float32
    D, F = D_MODEL, D_FF

    def nrm(k, shape, scale):
        return jax.random.normal(k, shape, f32) * scale

    def gain(k):
        return 1.0 + 0.05 * jax.random.normal(k, (DEPTH, D), f32)

    def bias(k):
        return 0.02 * jax.random.normal(k, (DEPTH, D), f32)

    return {
        "x": jax.random.normal(ks[0], (BATCH, SEQ, D), f32),
        "meta_tokens": nrm(ks[1], (N_META, D), 1.0),
        "ffn1_w_gu": nrm(ks[2], (DEPTH, D, 2 * F), D ** -0.5),
        "ffn1_w_down": nrm(ks[3], (DEPTH, F, D), BETA * F ** -0.5),
        "ln1_g": gain(ks[4]),
        "ln1_b": bias(ks[5]),
        "w_in": nrm(ks[6], (DEPTH, D, IN_COLS), D ** -0.5),
        "conv_w": nrm(ks[7], (DEPTH, CONV_K, CONV_CH), CONV_K ** -0.5),
        "pool_w": nrm(ks[8], (DEPTH, N_POOL_GROUPS, POOL_GROUP, POOL_GROUP), POOL_GROUP ** -0.5),
        "pool_scale": 1.0 + 0.1 * jax.random.normal(ks[9], (DEPTH, POOL_CH), f32),
        "w_out": nrm(ks[10], (DEPTH, MIX_WIDTH, D), BETA * MIX_WIDTH ** -0.5),
        "ln2_g": gain(ks[11]),
        "ln2_b": bias(ks[12]),
        "ffn2_w_gu": nrm(ks[13], (DEPTH, D, 2 * F), D ** -0.5),
        "ffn2_w_down": nrm(ks[14], (DEPTH, F, D), BETA * F ** -0.5),
        "ln3_g": gain(ks[15]),
        "ln3_b": bias(ks[16]),
    }


def reference(x, meta_tokens, ffn1_w_gu, ffn1_w_down, ln1_g, ln1_b, w_in, conv_w, pool_w,
              pool_scale, w_out, ln2_g, ln2_b, ffn2_w_gu, ffn2_w_down, ln3_g, ln3_b):
    b = x.shape[0]
    meta = jnp.broadcast_to(meta_tokens.astype(x.dtype)[None], (b, N_META, D_MODEL))
    h = jnp.concatenate([meta, x], axis=1)
    for i in range(DEPTH):
        h = layer_norm(ALPHA * h + 0.5 * swiglu_ffn(h, ffn1_w_gu[i], ffn1_w_down[i]),
                       ln1_g[i], ln1_b[i])
        h = layer_norm(ALPHA * h + hybrid_mixer(h, w_in[i], conv_w[i], pool_w[i],
                                                 pool_scale[i], w_out[i]),
                       ln2_g[i], ln2_b[i])
        h = layer_norm(ALPHA * h + 0.5 * swiglu_ffn(h, ffn2_w_gu[i], ffn2_w_down[i]),
                       ln3_g[i], ln3_b[i])
    return h[:, N_META:]
```

```python
import numpy as np
from contextlib import ExitStack
import concourse.bass as bass
import concourse.mybir as mybir
from concourse.bass_utils import run_bass_kernel_spmd

F32 = mybir.dt.float32
BF16 = mybir.dt.bfloat16
ALU = mybir.AluOpType
AF = mybir.ActivationFunctionType

D = 4096
F = 11008
NCH = D // 128
FCH = F // 128
PARTS = [(0, 30), (30, 58), (58, 86)]
ALPHA = 2.0 ** 0.25
EPS = 1e-5
NSLOT = 5
CELL = 528
NCORES = 8
TOK = 1024
HALO = 16

V_LN = 0
V_CW = 192
V_PS = 240
NV = 256


class Eng:
    def __init__(self, name, idx, is_queue=False):
        self.name = name
        self.idx = idx
        self.ops = []


class Prog:
    def __init__(self):
        self.engs = {}
        for i, n in enumerate(["pe", "act", "dve", "pool", "sp"]):
            self.engs[n] = Eng(n, i)
        self.state = {}
        self.dma_sem_count = {}

    @staticmethod
    def _merge(dst, tok):
        if tok is None:
            return
        src, val = tok
        if dst.get(src, -1) < val:
            dst[src] = val

    def op(self, eng, fn, reads=(), writes=(), dma_sem=None, extra_waits=()):
        e = self.engs[eng]
        waits = {}
        for t in extra_waits:
            self._merge(waits, t)
        wset = set(writes)
        for k in reads:
            if k in wset:
                continue
            st = self.state.get(k)
            if st:
                for s, v in st[0].items():
                    self._merge(waits, (s, v))
        for k in writes:
            st = self.state.get(k)
            if st:
                for s, v in st[0].items():
                    self._merge(waits, (s, v))
                for s, v in st[1].items():
                    self._merge(waits, (s, v))
        if eng == "pe":
            waits.pop(("e", "pe"), None)
        opi = len(e.ops)
        if dma_sem is not None:
            prev = self.dma_sem_count.get(dma_sem, 0)
            if prev > 0:
                self._merge(waits, (("d", dma_sem), prev))
            val = prev + 16
            self.dma_sem_count[dma_sem] = val
            tok = (("d", dma_sem), val)
        else:
            tok = (("e", eng), opi)
        for (s, v) in waits.items():
            if s[0] == "e":
                self.engs[s[1]].ops[v][2] = True
        e.ops.append([fn, sorted(waits.items(), key=lambda x: str(x[0])), False, dma_sem])
        for k in reads:
            if k in wset:
                continue
            st = self.state.setdefault(k, ({}, {}))
            self._merge(st[1], tok)
        for k in writes:
            self.state[k] = ({tok[0]: tok[1]}, {})
        return tok

    def alias(self, old_keys, new_keys):
        w = {}
        for k in old_keys:
            st = self.state.get(k)
            if st:
                for s, v in st[0].items():
                    self._merge(w, (s, v))
                for s, v in st[1].items():
                    self._merge(w, (s, v))
        for k in new_keys:
            st = self.state.get(k)
            ww = dict(w)
            if st:
                for s, v in st[0].items():
                    self._merge(ww, (s, v))
                for s, v in st[1].items():
                    self._merge(ww, (s, v))
            self.state[k] = (ww, {})

    def all_tokens(self, keys):
        w = {}
        for k in keys:
            st = self.state.get(k)
            if st:
                for s, v in st[0].items():
                    self._merge(w, (s, v))
                for s, v in st[1].items():
                    self._merge(w, (s, v))
        return list(w.items())

    def emit(self, eng, handle, sems, dma_sems):
        e = self.engs[eng]
        seen = {}
        for (fn, waits, needs_inc, dma_sem) in e.ops:
            for (s, v) in waits:
                if s[0] == "e":
                    val = self._count(s[1], v)
                    sem = sems[s[1]]
                else:
                    val = v
                    sem = dma_sems[s[1]]
                if seen.get(s, -1) >= val:
                    continue
                seen[s] = val
                handle.wait_ge(sem, val)
            inst = fn(handle)
            if dma_sem is not None:
                inst.then_inc(dma_sems[dma_sem], 16)
            elif needs_inc:
                inst.then_inc(sems[eng], 1)

    def finalize(self):
        self._pref = {}
        for n, e in self.engs.items():
            c = 0
            p = []
            for o in e.ops:
                if o[2] and o[3] is None:
                    c += 1
                p.append(c)
            self._pref[n] = p

    def _count(self, eng, opi):
        return self._pref[eng][opi]


def build_nc():
    nc = bass.Bass("TRN2", target_bir_lowering=False)
    dram = {}

    def din(name, shape):
        dram[name] = nc.dram_tensor(name, shape, F32, kind="ExternalInput").ap()
        return dram[name]

    xin = din("xin", [TOK + HALO, D])
    ident_d = din("ident", [128, 128])
    vecs_d = din("vecs", [128, NV])
    w_gu = [din("w_gu1", [2 * F // 256, D * 256]), din("w_gu2", [2 * F // 256, D * 256])]
    w_dn = [din("w_d1", [D // 256, F * 256]), din("w_d2", [D // 256, F * 256])]
    w_in = din("w_in", [8192 // 256, D * 256])
    pool_w = din("pool_w", [2, 2048 * 256])
    w_out = din("w_out", [D // 256, D * 256])
    yout = nc.dram_tensor("y", [TOK, D], F32, kind="ExternalOutput").ap()

    def kview(w):
        return w

    P = Prog()

    with ExitStack() as es:
        hacc = es.enter_context(nc.sbuf_tensor("hacc", [128, NCH, CELL], F32))
        hb = es.enter_context(nc.sbuf_tensor("hb", [128, NCH, CELL], BF16))
        gy = es.enter_context(nc.sbuf_tensor("gy", [128, 16384], BF16))
        ring = [es.enter_context(nc.sbuf_tensor(f"ring{i}", [128, 16, 256], BF16)) for i in range(NSLOT)]
        NCELL = 14
        scr = es.enter_context(nc.sbuf_tensor("scr", [128, NCELL, CELL], F32))
        ident = es.enter_context(nc.sbuf_tensor("identS", [128, 128], F32))
        ones = es.enter_context(nc.sbuf_tensor("onesS", [128, 128], F32))
        vecs = es.enter_context(nc.sbuf_tensor("vecsS", [128, NV], F32))
        avecs = es.enter_context(nc.sbuf_tensor("avecsS", [128, 128], F32))
        epsc = es.enter_context(nc.sbuf_tensor("epsc", [128, 1], F32))
        carry_cx = es.enter_context(nc.sbuf_tensor("carry_cx", [128, 16, 16], F32))
        carry_z = es.enter_context(nc.sbuf_tensor("carry_z", [128, 16, 16], F32))
        ps = es.enter_context(nc.psum_tensor("ps", [128, 8, 512], F32))

        G = gy[:, 0:31 * CELL].rearrange("p (c n) -> p c n", n=CELL)
        Y = gy[:, :].rearrange("p (c n) -> p c n", n=512)
        gyf = gy.bitcast(F32)
        STG = [gyf[:, 0:4096], gyf[:, 4096:8192]]

        gy_g_keys = [("G", c) for c in range(31)]
        gy_y_keys = [("Y", c) for c in range(32)]
        gy_s_keys = [("STG", i) for i in range(4)]

        C_S1, C_S2, C_SQ = 0, 1, 2
        C_SGT = [4, 6]
        C_MEAN, C_RSTD = 8, 9
        C_T1 = [10, 12]
        C_T2 = [11, 13]
        C_Z, C_SA, C_SB = 4, 6, 8
        C_DB = [10, 12]
        C_GC, C_CX, C_ACC = 4, 6, 8

        def cells(c0, n):
            return [("scr", c) for c in range(c0, c0 + n)]

        def scr2(c0):
            return scr[:, c0:c0 + 2, :]

        bank_ptr = [0]

        def alloc_banks(nb):
            p = bank_ptr[0]
            p = ((p + nb - 1) // nb) * nb
            if p + nb > 8:
                p = 0
            bank_ptr[0] = p + nb
            return p

        def bkeys(b0, nb):
            return [("ps", b) for b in range(b0, b0 + nb)]

        units = []
        slot_tok = {}

        first_unit_waits = []

        def add_unit(wv, kc0, nk, col0):
            ui = len(units)
            slot = ui % NSLOT
            ne = nk * 256
            ns = (ne + 2047) // 2048
            src = wv[col0 // 256, kc0 * 32768:(kc0 + nk) * 32768].rearrange("(p s l) -> p s l", p=128, s=ns)
            dst = ring[slot][:, 0:nk, :].rearrange("p k c -> p (k c)").rearrange("p (s l) -> p s l", s=ns)
            units.append((src, nk, slot))

            def fn(g, src=src, dst=dst):
                return g.dma_start(out=dst, in_=src)
            ew = list(first_unit_waits) if ui == 0 else []
            P.op("pool", fn, reads=(), writes=[("w", slot)], dma_sem=f"ring{slot}", extra_waits=ew)
            return slot

        def mm_units(wv, col0, kchunks, rhs_fn, rhs_keys_fn, subs, b0, split=False):
            nsub = len(subs)
            nk_tot = len(kchunks)
            thunks = []
            k0 = 0
            while k0 < nk_tot:
                nk = min(16, nk_tot - k0)
                cell = {}
                halves = [(0, nk // 2), (nk // 2, nk)] if (split and nk >= 2) else [(0, nk)]
                for hi, (ka, kb) in enumerate(halves):
                    def thunk(k0=k0, nk=nk, ka=ka, kb=kb, hi=hi, cell=cell):
                        if hi == 0:
                            cell["slot"] = add_unit(wv, kchunks[0] + k0, nk, col0)
                        slot = cell["slot"]

                        def fn(t):
                            inst = None
                            for kk in range(ka, kb):
                                for c in range(2):
                                    for si, (s0, w) in enumerate(subs):
                                        inst = t.matmul(ps[:, b0 + c * nsub + si, 0:w],
                                                        lhsT=ring[slot][:, kk, c * 128:(c + 1) * 128],
                                                        rhs=rhs_fn(k0 + kk, si),
                                                        start=(k0 + kk == 0),
                                                        stop=(k0 + kk == nk_tot - 1))
                            return inst
                        reads = [("w", slot)]
                        for kk in range(ka, kb):
                            reads += rhs_keys_fn(k0 + kk)
                        P.op("pe", fn, reads=reads, writes=bkeys(b0, 2 * nsub))
                    thunks.append(thunk)
                k0 += nk
            return thunks

        def mm_block(wv, col0, kchunks, rhs_fn, rhs_keys_fn, subs, b0):
            for th in mm_units(wv, col0, kchunks, rhs_fn, rhs_keys_fn, subs, b0):
                th()

        def run_interleaved(unit_lists):
            n = max(len(u) for u in unit_lists)
            for i in range(n):
                for u in unit_lists:
                    if i < len(u):
                        u[i]()

        def psv(b0, nsub, w):
            return ps[:, b0:b0 + 2 * nsub, 0:w].rearrange("p (c s) n -> p c s n", s=nsub)

        def colv(t3, ch0, subs):
            s0, w = subs[0]
            n = len(subs)
            return t3[:, ch0:ch0 + 2, s0:s0 + n * w].rearrange("p c (s n) -> p c s n", s=n)

        P.op("sp", lambda q: q.dma_start(out=ident[:], in_=ident_d[:, :]), writes=[("ident",)], dma_sem="sp0")
        P.op("sp", lambda q: q.dma_start(out=vecs[:], in_=vecs_d[:, :]), writes=[("vecs",)], dma_sem="sp1")
        P.op("dve", lambda v: v.memset(ones[:], 1.0), writes=[("ones",)])
        P.op("dve", lambda v: v.memset(epsc[:], EPS), writes=[("epsc",)])
        P.op("dve", lambda v: v.tensor_scalar_mul(out=avecs[:], in0=vecs[:, 0:128], scalar1=ALPHA),
             reads=[("vecs",)], writes=[("avecs",)])

        sp_sem_rr = [0]

        def sp_sem():
            i = sp_sem_rr[0]
            sp_sem_rr[0] = (i + 1) % 8
            return f"sp{i}"

        def stats_accum(pair, c0, T):
            ch = 2 * pair
            sq = scr2(C_SQ)
            P.op("act", lambda a: a.activation(out=sq[:, :, c0:c0 + T], in_=hacc[:, ch:ch + 2, c0:c0 + T],
                                               func=AF.Square),
                 reads=[("hacc", ch), ("hacc", ch + 1)], writes=cells(C_SQ, 2))
            s1 = scr[:, C_S1, c0:c0 + T]
            s2 = scr[:, C_S2, c0:c0 + T]
            if pair == 0:
                P.op("dve", lambda v: v.tensor_tensor(out=s1, in0=hacc[:, ch, c0:c0 + T],
                                                      in1=hacc[:, ch + 1, c0:c0 + T], op=ALU.add),
                     reads=[("hacc", ch), ("hacc", ch + 1)], writes=cells(C_S1, 1))
                P.op("dve", lambda v: v.tensor_tensor(out=s2, in0=sq[:, 0, c0:c0 + T],
                                                      in1=sq[:, 1, c0:c0 + T], op=ALU.add),
                     reads=cells(C_SQ, 2), writes=cells(C_S2, 1))
            else:
                for j in range(2):
                    P.op("dve", lambda v, j=j: v.tensor_tensor(out=s1, in0=s1, in1=hacc[:, ch + j, c0:c0 + T],
                                                               op=ALU.add),
                         reads=[("hacc", ch + j)], writes=cells(C_S1, 1))
                for j in range(2):
                    P.op("dve", lambda v, j=j: v.tensor_tensor(out=s2, in0=s2, in1=sq[:, j, c0:c0 + T],
                                                               op=ALU.add),
                         reads=cells(C_SQ, 2), writes=cells(C_S2, 1))

        def ln_finish(vg, vb, c0, T, final, after_chunk=None):
            subs = [(c0, T // 2), (c0 + T // 2, T // 2)]
            b0 = alloc_banks(4)

            def fn(t):
                inst = None
                for i, cell in enumerate((C_S1, C_S2)):
                    for si, (s0, w) in enumerate(subs):
                        inst = t.matmul(ps[:, b0 + 2 * i + si, 0:w], lhsT=ones[:, :], rhs=scr[:, cell, s0:s0 + w],
                                        start=True, stop=True)
                return inst
            P.op("pe", fn, reads=cells(C_S1, 2) + [("ones",)], writes=bkeys(b0, 4))
            w2 = T // 2
            mean = scr[:, C_MEAN, c0:c0 + T]
            rstd = scr[:, C_RSTD, c0:c0 + T]
            tmp = scr[:, C_T1[0], c0:c0 + T]
            mean2 = mean.rearrange("p (s n) -> p s n", s=2)
            tmp2 = tmp.rearrange("p (s n) -> p s n", s=2)
            P.op("act", lambda a: a.activation(out=mean2, in_=ps[:, b0:b0 + 2, 0:w2], func=AF.Copy, scale=1.0 / D),
                 reads=bkeys(b0, 2), writes=cells(C_MEAN, 1))
            P.op("act", lambda a: a.activation(out=tmp2, in_=ps[:, b0:b0 + 2, 0:w2], func=AF.Square, scale=1.0 / D),
                 reads=bkeys(b0, 2), writes=cells(C_T1[0], 1))
            P.op("dve", lambda v: v.scalar_tensor_tensor(out=tmp2, in0=ps[:, b0 + 2:b0 + 4, 0:w2], scalar=1.0 / D,
                                                         in1=tmp2, op0=ALU.mult, op1=ALU.subtract),
                 reads=bkeys(b0 + 2, 2), writes=cells(C_T1[0], 1))
            P.op("act", lambda a: a.activation(out=tmp, in_=tmp, func=AF.Sqrt, bias=epsc[:, 0:1]),
                 reads=[("epsc",)], writes=cells(C_T1[0], 1))
            P.op("dve", lambda v: v.reciprocal(out=rstd, in_=tmp),
                 reads=cells(C_T1[0], 1), writes=cells(C_RSTD, 1))
            for c in range(NCH):
                t1 = scr[:, C_T1[c % 2], c0:c0 + T]
                t2 = scr[:, C_T2[c % 2], c0:c0 + T]
                hc = hacc[:, c, c0:c0 + T]
                P.op("dve", lambda v, t1=t1, hc=hc: v.tensor_tensor(out=t1, in0=hc, in1=mean, op=ALU.subtract),
                     reads=[("hacc", c)] + cells(C_MEAN, 1), writes=cells(C_T1[c % 2], 1))
                P.op("dve", lambda v, t1=t1, t2=t2: v.tensor_tensor(out=t2, in0=t1, in1=rstd, op=ALU.mult),
                     reads=cells(C_T1[c % 2], 1) + cells(C_RSTD, 1), writes=cells(C_T2[c % 2], 1))
                if final:
                    P.op("act", lambda a, t2=t2, hc=hc, c=c: a.activation(
                        out=hc, in_=t2, func=AF.Identity, bias=vecs[:, vb + c:vb + c + 1],
                        scale=vecs[:, vg + c:vg + c + 1]),
                        reads=cells(C_T2[c % 2], 1) + [("vecs",)], writes=[("hacc", c)])
                else:
                    P.op("act", lambda a, t2=t2, c=c: a.activation(
                        out=hb[:, c, c0:c0 + T], in_=t2, func=AF.Identity, bias=vecs[:, vb + c:vb + c + 1],
                        scale=vecs[:, vg + c:vg + c + 1]),
                        reads=cells(C_T2[c % 2], 1) + [("vecs",)], writes=[("hb", c)])
                    P.op("act", lambda a, t2=t2, hc=hc, c=c: a.activation(
                        out=hc, in_=t2, func=AF.Identity, bias=avecs[:, vb + c:vb + c + 1],
                        scale=avecs[:, vg + c:vg + c + 1]),
                        reads=cells(C_T2[c % 2], 1) + [("avecs",)], writes=[("hacc", c)])
                if after_chunk is not None:
                    after_chunk(c)

        def ffn(li, subs, c0, T):
            wg = kview(w_gu[li])
            wd = kview(w_dn[li])
            nsub = len(subs)
            w = subs[0][1]
            P.alias(gy_y_keys + gy_s_keys + [("SX", i) for i in range(3)] + [("SX", i, "h") for i in range(3)], gy_g_keys)
            for pi, (f0, f1) in enumerate(PARTS):
                npair = (f1 - f0) // 2
                ni = (2 if nsub == 1 else 1) if pi == 0 else 1
                q = 0
                while q < npair:
                    grp = list(range(q, min(q + ni, npair))) if q == 0 else [q]
                    info = []
                    ulists = []
                    for qq in grp:
                        j0 = f0 + 2 * qq
                        bg = alloc_banks(2 * nsub)
                        bu = alloc_banks(2 * nsub)
                        spl = (pi == 0 and q == 0)
                        ulists.append(mm_units(wg, j0 * 128, list(range(NCH)),
                                               lambda k, si: hb[:, k, subs[si][0]:subs[si][0] + w],
                                               lambda k: [("hb", k)], subs, bg, split=spl))
                        ulists.append(mm_units(wg, F + j0 * 128, list(range(NCH)),
                                               lambda k, si: hb[:, k, subs[si][0]:subs[si][0] + w],
                                               lambda k: [("hb", k)], subs, bu, split=spl))
                        info.append((qq, bg, bu))
                    if pi == 0 and q == 0:
                        run_interleaved(ulists)
                    else:
                        for u in ulists:
                            for th in u:
                                th()
                    for (qq, bg, bu) in info:
                        jl = 2 * qq
                        sgt = scr[:, C_SGT[qq % 2]:C_SGT[qq % 2] + 2, :].rearrange("p a n -> p (a n)")[:, 0:2 * nsub * w] \
                            .rearrange("p (c s n) -> p c s n", c=2, s=nsub)
                        P.op("act", lambda a, bg=bg, sgt=sgt: a.activation(out=sgt, in_=psv(bg, nsub, w), func=AF.Silu),
                             reads=bkeys(bg, 2 * nsub), writes=cells(C_SGT[qq % 2], 2))
                        gv = colv(G, jl, subs)
                        P.op("dve", lambda v, bu=bu, sgt=sgt, gv=gv: v.tensor_tensor(out=gv, in0=psv(bu, nsub, w),
                                                                                    in1=sgt, op=ALU.mult),
                             reads=bkeys(bu, 2 * nsub) + cells(C_SGT[qq % 2], 2), writes=[("G", jl), ("G", jl + 1)])
                    q += len(grp)
                last_part = (pi == len(PARTS) - 1)
                for i in range(NCH // 2):
                    bd = alloc_banks(2 * nsub)
                    mm_block(wd, i * 256, list(range(f0, f1)),
                             lambda k, si: G[:, k, subs[si][0]:subs[si][0] + w],
                             lambda k: [("G", k)], subs, bd)
                    hv = colv(hacc, 2 * i, subs)
                    P.op("dve", lambda v, bd=bd, hv=hv: v.scalar_tensor_tensor(
                        out=hv, in0=psv(bd, nsub, w), scalar=0.5, in1=hv, op0=ALU.mult, op1=ALU.add),
                        reads=bkeys(bd, 2 * nsub), writes=[("hacc", 2 * i), ("hacc", 2 * i + 1)])
                    if last_part:
                        stats_accum(i, c0, T)

        def mixer(t, subs1, off):
            wi = kview(w_in)
            wo = kview(w_out)
            pw = kview(pool_w)
            boff = HALO - off
            ns1 = len(subs1)
            w1 = subs1[0][1]
            T1 = ns1 * w1 if ns1 > 1 else w1
            subs2 = [(off, 512)]
            P.alias(gy_g_keys + gy_s_keys + [("SX", i) for i in range(3)], gy_y_keys)

            def hb1(k, si):
                return hb[:, k, subs1[si][0]:subs1[si][0] + w1]

            def hb2(k, si):
                return hb[:, k, off:off + 512]

            def hbk(k):
                return [("hb", k)]

            def buf_view(c0cell):
                b = scr2(c0cell)
                s0 = subs1[0][0] + boff
                return b[:, :, s0:s0 + ns1 * w1].rearrange("p c (s n) -> p c s n", s=ns1)

            def z_units(q, bz, split=False):
                return mm_units(wi, 6144 + 256 * q, list(range(NCH)), hb1, hbk, subs1, bz, split=split)

            def z_block(q, bz=None):
                g = q // 2
                if bz is None:
                    bz = alloc_banks(2 * ns1)
                    for th in z_units(q, bz):
                        th()
                z = scr2(C_Z)
                P.op("act", lambda a: a.activation(out=buf_view(C_Z), in_=psv(bz, ns1, w1), func=AF.Copy),
                     reads=bkeys(bz, 2 * ns1), writes=cells(C_Z, 2))
                if t == 0:
                    P.op("act", lambda a: a.activation(out=carry_z[:, 2 * q:2 * q + 2, :], in_=z[:, :, 512:528],
                                                       func=AF.Copy),
                         reads=cells(C_Z, 2), writes=[("carry_z", q)])
                else:
                    P.op("act", lambda a: a.activation(out=z[:, :, 0:16], in_=carry_z[:, 2 * q:2 * q + 2, :],
                                                       func=AF.Copy),
                         reads=[("carry_z", q)], writes=cells(C_Z, 2))
                cur, curc = z, C_Z
                lo = 0
                bufs = [(scr2(C_SA), C_SA), (scr2(C_SB), C_SB)]
                bi = 0
                for step in (1, 2, 4, 8)[:g + 1]:
                    nxt, nxtc = bufs[bi]
                    bi ^= 1
                    P.op("dve", lambda v, cur=cur, nxt=nxt, lo=lo, step=step: v.tensor_tensor(
                        out=nxt[:, :, lo + step:528], in0=cur[:, :, lo + step:528], in1=cur[:, :, lo:528 - step],
                        op=ALU.add),
                        reads=cells(curc, 2), writes=cells(nxtc, 2))
                    cur, curc = nxt, nxtc
                    lo += step
                W = 2 ** (g + 1)
                dbc = C_DB[g % 2]
                db = scr[:, dbc:dbc + 2, :].rearrange("p a n -> p (a n)").bitcast(BF16)[:, 0:2048] \
                    .rearrange("p (c n) -> p c n", n=512)
                dsl = db[:, 2 * (q % 2):2 * (q % 2) + 2, :]
                P.op("dve", lambda v, cur=cur: v.scalar_tensor_tensor(
                    out=dsl, in0=cur[:, :, 16:528], scalar=1.0 / W, in1=z[:, :, 16:528],
                    op0=ALU.mult, op1=ALU.subtract),
                    reads=cells(curc, 2) + cells(C_Z, 2), writes=[("db", g % 2, q % 2)])

            def p_blocks(g):
                dbc = C_DB[g % 2]
                db = scr[:, dbc:dbc + 2, :].rearrange("p a n -> p (a n)").bitcast(BF16)[:, 0:2048] \
                    .rearrange("p (c n) -> p c n", n=512)
                for ob in range(2):
                    bp = alloc_banks(2)
                    mm_block(pw, 256 * ob, list(range(4 * g, 4 * g + 4)),
                             lambda k, si: db[:, k, :],
                             lambda k: [("db", g % 2, k // 2)], [(0, 512)], bp)
                    for c in range(2):
                        ch = 4 * g + 2 * ob + c
                        P.op("act", lambda a, bp=bp, c=c, ch=ch: a.activation(
                            out=Y[:, 16 + ch, :], in_=ps[:, bp + c, 0:512], func=AF.Identity,
                            scale=vecs[:, V_PS + ch:V_PS + ch + 1]),
                            reads=bkeys(bp + c, 1) + [("vecs",)], writes=[("Y", 16 + ch)])

            def db_guard_begin():
                P.alias(cells(C_DB[0], 4), [("db", a, b) for a in range(2) for b in range(2)])

            def db_guard_end():
                P.alias([("db", a, b) for a in range(2) for b in range(2)], cells(C_DB[0], 4))

            db_guard_begin()
            order = [("z", 0), ("z", 1), ("z", 2), ("z", 3), ("p", 0), ("z", 4), ("z", 5), ("p", 1),
                     ("z", 6), ("z", 7), ("p", 2)]
            nz = 4 if ns1 == 1 else 2
            zb = [alloc_banks(2 * ns1) for _ in range(nz)]
            run_interleaved([z_units(i, zb[i], split=True) for i in range(nz)])
            for kind, i in order:
                if kind == "z":
                    z_block(i, zb[i] if i < nz else None)
                else:
                    p_blocks(i)

            def conv_triple(cp):
                bc = alloc_banks(2 * ns1)
                mm_block(wi, 2048 + 256 * cp, list(range(NCH)), hb1, hbk, subs1, bc)
                P.op("act", lambda a: a.activation(out=buf_view(C_GC), in_=psv(bc, ns1, w1), func=AF.Copy),
                     reads=bkeys(bc, 2 * ns1), writes=cells(C_GC, 2))
                bx = alloc_banks(2 * ns1)
                mm_block(wi, 4096 + 256 * cp, list(range(NCH)), hb1, hbk, subs1, bx)
                cx = scr2(C_CX)
                P.op("dve", lambda v: v.tensor_tensor(out=buf_view(C_CX), in0=psv(bx, ns1, w1), in1=buf_view(C_GC),
                                                      op=ALU.mult),
                     reads=bkeys(bx, 2 * ns1) + cells(C_GC, 2), writes=cells(C_CX, 2))
                if t == 0:
                    P.op("act", lambda a: a.activation(out=carry_cx[:, 2 * cp:2 * cp + 2, :], in_=cx[:, :, 512:528],
                                                       func=AF.Copy),
                         reads=cells(C_CX, 2), writes=[("carry_cx", cp)])
                else:
                    P.op("act", lambda a: a.activation(out=cx[:, :, 0:16], in_=carry_cx[:, 2 * cp:2 * cp + 2, :],
                                                       func=AF.Copy),
                         reads=[("carry_cx", cp)], writes=cells(C_CX, 2))
                bb = alloc_banks(2)
                mm_block(wi, 256 * cp, list(range(NCH)), hb2, hbk, subs2, bb)
                acc = scr[:, C_ACC, 0:512]
                for c in range(2):
                    ch = 2 * cp + c

                    def wcol(kk, ch=ch):
                        return vecs[:, V_CW + 16 * kk + ch:V_CW + 16 * kk + ch + 1]
                    P.op("dve", lambda v, c=c, wcol=wcol: v.tensor_scalar_mul(out=acc, in0=cx[:, c, 14:526],
                                                                              scalar1=wcol(0)),
                         reads=cells(C_CX, 2) + [("vecs",)], writes=cells(C_ACC, 1))
                    P.op("dve", lambda v, c=c, wcol=wcol: v.scalar_tensor_tensor(
                        out=acc, in0=cx[:, c, 15:527], scalar=wcol(1), in1=acc, op0=ALU.mult, op1=ALU.add),
                        reads=cells(C_CX, 2) + [("vecs",)], writes=cells(C_ACC, 1))
                    P.op("dve", lambda v, c=c, wcol=wcol: v.scalar_tensor_tensor(
                        out=acc, in0=cx[:, c, 16:528], scalar=wcol(2), in1=acc, op0=ALU.mult, op1=ALU.add),
                        reads=cells(C_CX, 2) + [("vecs",)], writes=cells(C_ACC, 1))
                    P.op("dve", lambda v, c=c, ch=ch: v.tensor_tensor(out=Y[:, ch, :], in0=ps[:, bb + c, 0:512],
                                                                      in1=acc, op=ALU.mult),
                         reads=bkeys(bb + c, 1) + cells(C_ACC, 1), writes=[("Y", ch)])

            conv_triple(0)
            p_blocks(3)
            db_guard_end()
            for cp in range(1, 8):
                conv_triple(cp)

            for i in range(NCH // 2):
                bo = alloc_banks(2)
                mm_block(wo, i * 256, list(range(NCH)), lambda k, si: Y[:, k, :], lambda k: [("Y", k)],
                         [(0, 512)], bo)
                hv = hacc[:, 2 * i:2 * i + 2, off:off + 512]
                P.op("dve", lambda v, bo=bo, hv=hv: v.tensor_tensor(out=hv, in0=ps[:, bo:bo + 2, 0:512], in1=hv,
                                                                    op=ALU.add),
                     reads=bkeys(bo, 2), writes=[("hacc", 2 * i), ("hacc", 2 * i + 1)])
                stats_accum(i, off, 512)

        gy_sx_keys = [("SX", i) for i in range(3)]
        gy_all_keys = gy_g_keys + gy_y_keys + gy_s_keys + gy_sx_keys

        def slab_load(t, c4, stg, key):
            row0 = 0 if t == 0 else 528
            src = xin[row0:row0 + 512, c4 * 512:(c4 + 1) * 512].rearrange("(tb p) f -> p tb f", p=128)
            tok = P.op("sp", lambda q: q.dma_start(out=stg[:, 0:4, :], in_=src), writes=[key], dma_sem=sp_sem())
            if t == 0 and c4 < 6:
                first_unit_waits.append(tok)
            if t == 0:
                src2 = xin[512:528, c4 * 512:(c4 + 1) * 512]
                P.op("sp", lambda q: q.dma_start(out=stg[0:16, 4, :], in_=src2), writes=[key + ("h",)],
                     dma_sem=sp_sem())

        def slab_consume(t, c4, stg, key, defer=None):
            nblk = 5 if t == 0 else 4
            for tb in range(nblk):
                nt = 128 if tb < 4 else 16
                b = alloc_banks(1)
                rk = key if tb < 4 else key + ("h",)

                def fn(tt, b=b, nt=nt, tb=tb):
                    inst = None
                    for qq in range(4):
                        inst = tt.transpose(out=ps[:, b, qq * 128:qq * 128 + nt],
                                            in_=stg[0:nt, tb, qq * 128:(qq + 1) * 128], identity=ident[0:nt, 0:nt])
                    return inst
                P.op("pe", fn, reads=[rk, ("ident",)], writes=bkeys(b, 1))
                pv = ps[:, b, :].rearrange("p (q n) -> p q n", n=128)[:, :, 0:nt]
                col = tb * 128

                def ev(pv=pv, col=col, nt=nt, b=b):
                    P.op("act", lambda a: a.activation(
                        out=hacc[:, 4 * c4:4 * c4 + 4, col:col + nt], in_=pv, func=AF.Copy, scale=ALPHA),
                        reads=bkeys(b, 1), writes=[("hacc", 4 * c4 + i) for i in range(4)])
                    P.op("dve", lambda v: v.tensor_copy(
                        out=hb[:, 4 * c4:4 * c4 + 4, col:col + nt], in_=pv),
                        reads=[], writes=bkeys(b, 1) + [("hb", 4 * c4 + i) for i in range(4)])
                if defer is None:
                    ev()
                else:
                    defer.append(ev)

        def startup_prologue():
            P.alias(gy_all_keys, gy_sx_keys + [k + ("h",) for k in gy_sx_keys])
            bufs = [gyf[:, i * 2560:(i + 1) * 2560].rearrange("p (tb f) -> p tb f", f=512) for i in range(3)]
            for c4 in range(3):
                slab_load(0, c4, bufs[c4 % 3], ("SX", c4 % 3))
            for c4 in range(8):
                slab_consume(0, c4, bufs[c4 % 3], ("SX", c4 % 3))
                if c4 + 3 < 8:
                    slab_load(0, c4 + 3, bufs[c4 % 3], ("SX", c4 % 3))

        out_toks = []

        def epilogue_slab(t, off, c4, defer=None):
            sb = c4 % 2
            stg = gyf[:, sb * 2048:(sb + 1) * 2048].rearrange("p (tb f) -> p tb f", f=512)
            rows = yout[t * 512:(t + 1) * 512, :].rearrange("(tb p) f -> p tb f", p=128)
            for tb in range(4):
                col = off + tb * 128
                b = alloc_banks(1)

                def fn(tt, b=b, col=col):
                    inst = None
                    for qq in range(4):
                        c = 4 * c4 + qq
                        inst = tt.transpose(out=ps[:, b, qq * 128:(qq + 1) * 128],
                                            in_=hacc[:, c, col:col + 128], identity=ident[:, :])
                    return inst
                P.op("pe", fn, reads=[("hacc", 4 * c4 + i) for i in range(4)] + [("ident",)], writes=bkeys(b, 1))
                dst = stg[:, tb, :]

                def ev(b=b, dst=dst, tb=tb):
                    if tb % 2 == 0:
                        P.op("act", lambda a: a.activation(out=dst, in_=ps[:, b, :], func=AF.Copy),
                             reads=bkeys(b, 1), writes=[("STG", sb)])
                    else:
                        P.op("dve", lambda v: v.tensor_copy(out=dst, in_=ps[:, b, :]),
                             reads=bkeys(b, 1), writes=[("STG", sb)])
                if defer is None:
                    ev()
                else:
                    defer.append(ev)

            def store():
                tok = P.op("sp", lambda q: q.dma_start(out=rows[:, :, c4 * 512:(c4 + 1) * 512], in_=stg),
                           reads=[("STG", sb)], dma_sem=sp_sem())
                out_toks.append(tok)
            if defer is None:
                store()
            else:
                defer.append(store)

        def boundary(t, off, nxt):
            P.alias(gy_all_keys, gy_s_keys)
            xb = [gyf[:, (2 + i) * 2048:(3 + i) * 2048].rearrange("p (tb f) -> p tb f", f=512) for i in range(2)]
            if nxt is not None:
                for c4 in range(2):
                    slab_load(nxt, c4, xb[c4 % 2], ("STG", 2 + c4 % 2))

            pending = []

            def cb(c):
                if c % 4 != 3:
                    return
                c4 = c // 4
                prev = list(pending)
                del pending[:]
                for ev in prev:
                    ev()
                epilogue_slab(t, off, c4, defer=pending)
                if nxt is not None and c4 >= 1:
                    slab_consume(nxt, c4 - 1, xb[(c4 - 1) % 2], ("STG", 2 + (c4 - 1) % 2), defer=pending)
                    if c4 + 1 < 8:
                        slab_load(nxt, c4 + 1, xb[(c4 + 1) % 2], ("STG", 2 + (c4 + 1) % 2))
            ln_finish(V_LN + 128, V_LN + 160, off, 512, final=True, after_chunk=cb)
            for ev in pending:
                ev()
            if nxt is not None:
                slab_consume(nxt, 7, xb[1], ("STG", 3))

        startup_prologue()
        for t in range(2):
            if t == 0:
                subs1 = [(0, 264), (264, 264)]
                off, T1 = 16, 528
            else:
                subs1 = [(0, 512)]
                off, T1 = 0, 512
            subs2 = [(off, 512)]
            ffn(0, subs1, 0, T1)
            ln_finish(V_LN + 0, V_LN + 32, 0, T1, final=False)
            mixer(t, subs1, off)
            ln_finish(V_LN + 64, V_LN + 96, off, 512, final=False)
            ffn(1, subs2, off, 512)
            boundary(t, off, 1 if t == 0 else None)

        P.op("sp", lambda q: q.nop(), extra_waits=out_toks + P.all_tokens([("STG", i) for i in range(4)]))

        P.finalize()

        sems = {n: es.enter_context(nc.semaphore(f"sem_{n}")) for n in P.engs}
        dma_sems = {n: es.enter_context(nc.semaphore(f"dsem_{n}")) for n in P.dma_sem_count}
        block = es.enter_context(nc.Block())

        @block.tensor
        def _(h):
            P.emit("pe", h, sems, dma_sems)

        @block.scalar
        def _(h):
            P.emit("act", h, sems, dma_sems)

        @block.vector
        def _(h):
            P.emit("dve", h, sems, dma_sems)

        @block.gpsimd
        def _(h):
            P.emit("pool", h, sems, dma_sems)

        @block.sync
        def _(h):
            P.emit("sp", h, sems, dma_sems)

    return nc


_NC_CACHE = {}


def kernel(x, meta_tokens, ffn1_w_gu, ffn1_w_down, ln1_g, ln1_b, w_in, conv_w, pool_w, pool_scale, w_out,
           ln2_g, ln2_b, ffn2_w_gu, ffn2_w_down, ln3_g, ln3_b):
    f32 = np.float32
    x = np.asarray(x, dtype=f32)
    meta = np.asarray(meta_tokens, dtype=f32)
    B = x.shape[0]

    def col(v):
        v = np.asarray(v, dtype=f32).reshape(-1)
        return v.reshape(-1, 128).T

    vecs = np.zeros((128, NV), dtype=f32)
    for i, v in enumerate([ln1_g, ln1_b, ln2_g, ln2_b, ln3_g, ln3_b]):
        vecs[:, V_LN + 32 * i:V_LN + 32 * (i + 1)] = col(v)
    cw = np.asarray(conv_w, dtype=f32).reshape(3, 2048)
    for k in range(3):
        vecs[:, V_CW + 16 * k:V_CW + 16 * (k + 1)] = col(cw[k])
    vecs[:, V_PS:V_PS + 16] = col(pool_scale)
    ident = np.eye(128, dtype=f32)

    def tile_w(w, K, N, units):
        w4 = np.asarray(w, dtype=f32).reshape(K // 128, 128, N // 256, 256)
        out = np.empty((N // 256, K * 256), dtype=f32)
        for (k0, nk) in units:
            blk = w4[k0:k0 + nk].transpose(2, 1, 0, 3)
            out[:, k0 * 32768:(k0 + nk) * 32768] = blk.reshape(N // 256, 128 * nk * 256)
        return out

    u32 = [(0, 16), (16, 16)]
    udn = []
    for (f0, f1) in PARTS:
        udn += [(f0, 16), (f0 + 16, f1 - f0 - 16)]
    upl = [(0, 4), (4, 4), (8, 4), (12, 4)]
    shared = {
        "ident": ident,
        "vecs": vecs,
        "w_gu1": tile_w(ffn1_w_gu, D, 2 * F, u32),
        "w_gu2": tile_w(ffn2_w_gu, D, 2 * F, u32),
        "w_d1": tile_w(ffn1_w_down, F, D, udn),
        "w_d2": tile_w(ffn2_w_down, F, D, udn),
        "w_in": tile_w(w_in, D, 8192, u32),
        "pool_w": tile_w(pool_w, 2048, 512, upl),
        "w_out": tile_w(w_out, D, D, u32),
    }
    in_maps = []
    for core in range(NCORES):
        b, half = core // 2, core % 2
        halo = meta if half == 0 else x[b, TOK - HALO:TOK]
        xin = np.concatenate([halo, x[b, half * TOK:(half + 1) * TOK]], axis=0)
        m = dict(shared)
        m["xin"] = np.ascontiguousarray(xin)
        in_maps.append(m)

    if "nc" not in _NC_CACHE:
        _NC_CACHE["nc"] = build_nc()
    nc = _NC_CACHE["nc"]
    res = run_bass_kernel_spmd(nc, in_maps, core_ids=list(range(NCORES)))
    out = np.empty((B, 2 * TOK, D), dtype=f32)
    for core in range(NCORES):
        b, half = core // 2, core % 2
        out[b, half * TOK:(half + 1) * TOK] = res.results[core]["y"]
    return out
```

```python
import numpy as np
from contextlib import ExitStack
import concourse.bass as bass
import concourse.mybir as mybir
from concourse.bass_utils import run_bass_kernel_spmd

F32 = mybir.dt.float32
BF16 = mybir.dt.bfloat16
ALU = mybir.AluOpType
AF = mybir.ActivationFunctionType

D = 4096
F = 11008
NCH = D // 128
FCH = F // 128
PARTS = [(0, 30), (30, 58), (58, 86)]
ALPHA = 2.0 ** 0.25
EPS = 1e-5
NSLOT = 5
CELL = 528
NCORES = 8
TOK = 1024
HALO = 16

V_LN = 0
V_CW = 192
V_PS = 240
NV = 256


class Eng:
    def __init__(self, name, idx, is_queue=False):
        self.name = name
        self.idx = idx
        self.ops = []


class Prog:
    def __init__(self):
        self.engs = {}
        for i, n in enumerate(["pe", "act", "dve", "pool", "sp"]):
            self.engs[n] = Eng(n, i)
        self.state = {}
        self.dma_sem_count = {}

    @staticmethod
    def _merge(dst, tok):
        if tok is None:
            return
        src, val = tok
        if dst.get(src, -1) < val:
            dst[src] = val

    def op(self, eng, fn, reads=(), writes=(), dma_sem=None, extra_waits=()):
        e = self.engs[eng]
        waits = {}
        for t in extra_waits:
            self._merge(waits, t)
        wset = set(writes)
        for k in reads:
            if k in wset:
                continue
            st = self.state.get(k)
            if st:
                for s, v in st[0].items():
                    self._merge(waits, (s, v))
        for k in writes:
            st = self.state.get(k)
            if st:
                for s, v in st[0].items():
                    self._merge(waits, (s, v))
                for s, v in st[1].items():
                    self._merge(waits, (s, v))
        if eng == "pe":
            waits.pop(("e", "pe"), None)
        opi = len(e.ops)
        if dma_sem is not None:
            prev = self.dma_sem_count.get(dma_sem, 0)
            if prev > 0:
                self._merge(waits, (("d", dma_sem), prev))
            val = prev + 16
            self.dma_sem_count[dma_sem] = val
            tok = (("d", dma_sem), val)
        else:
            tok = (("e", eng), opi)
        for (s, v) in waits.items():
            if s[0] == "e":
                self.engs[s[1]].ops[v][2] = True
        e.ops.append([fn, sorted(waits.items(), key=lambda x: str(x[0])), False, dma_sem])
        for k in reads:
            if k in wset:
                continue
            st = self.state.setdefault(k, ({}, {}))
            self._merge(st[1], tok)
        for k in writes:
            self.state[k] = ({tok[0]: tok[1]}, {})
        return tok

    def alias(self, old_keys, new_keys):
        w = {}
        for k in old_keys:
            st = self.state.get(k)
            if st:
                for s, v in st[0].items():
                    self._merge(w, (s, v))
                for s, v in st[1].items():
                    self._merge(w, (s, v))
        for k in new_keys:
            st = self.state.get(k)
            ww = dict(w)
            if st:
                for s, v in st[0].items():
                    self._merge(ww, (s, v))
                for s, v in st[1].items():
                    self._merge(ww, (s, v))
            self.state[k] = (ww, {})

    def all_tokens(self, keys):
        w = {}
        for k in keys:
            st = self.state.get(k)
            if st:
                for s, v in st[0].items():
                    self._merge(w, (s, v))
                for s, v in st[1].items():
                    self._merge(w, (s, v))
        return list(w.items())

    def emit(self, eng, handle, sems, dma_sems):
        e = self.engs[eng]
        seen = {}
        for (fn, waits, needs_inc, dma_sem) in e.ops:
            for (s, v) in waits:
                if s[0] == "e":
                    val = self._count(s[1], v)
                    sem = sems[s[1]]
                else:
                    val = v
                    sem = dma_sems[s[1]]
                if seen.get(s, -1) >= val:
                    continue
                seen[s] = val
                handle.wait_ge(sem, val)
            inst = fn(handle)
            if dma_sem is not None:
                inst.then_inc(dma_sems[dma_sem], 16)
            elif needs_inc:
                inst.then_inc(sems[eng], 1)

    def finalize(self):
        self._pref = {}
        for n, e in self.engs.items():
            c = 0
            p = []
            for o in e.ops:
                if o[2] and o[3] is None:
                    c += 1
                p.append(c)
            self._pref[n] = p

    def _count(self, eng, opi):
        return self._pref[eng][opi]


def build_nc():
    nc = bass.Bass("TRN2", target_bir_lowering=False)
    dram = {}

    def din(name, shape):
        dram[name] = nc.dram_tensor(name, shape, F32, kind="ExternalInput").ap()
        return dram[name]

    xin = din("xin", [TOK + HALO, D])
    ident_d = din("ident", [128, 128])
    vecs_d = din("vecs", [128, NV])
    w_gu = [din("w_gu1", [2 * F // 256, D * 256]), din("w_gu2", [2 * F // 256, D * 256])]
    w_dn = [din("w_d1", [D // 256, F * 256]), din("w_d2", [D // 256, F * 256])]
    w_in = din("w_in", [8192 // 256, D * 256])
    pool_w = din("pool_w", [2, 2048 * 256])
    w_out = din("w_out", [D // 256, D * 256])
    yout = nc.dram_tensor("y", [TOK, D], F32, kind="ExternalOutput").ap()

    def kview(w):
        return w

    P = Prog()

    with ExitStack() as es:
        hacc = es.enter_context(nc.sbuf_tensor("hacc", [128, NCH, CELL], F32))
        hb = es.enter_context(nc.sbuf_tensor("hb", [128, NCH, CELL], BF16))
        gy = es.enter_context(nc.sbuf_tensor("gy", [128, 16384], BF16))
        ring = [es.enter_context(nc.sbuf_tensor(f"ring{i}", [128, 16, 256], BF16)) for i in range(NSLOT)]
        NCELL = 14
        scr = es.enter_context(nc.sbuf_tensor("scr", [128, NCELL, CELL], F32))
        ident = es.enter_context(nc.sbuf_tensor("identS", [128, 128], F32))
        ones = es.enter_context(nc.sbuf_tensor("onesS", [128, 128], F32))
        vecs = es.enter_context(nc.sbuf_tensor("vecsS", [128, NV], F32))
        avecs = es.enter_context(nc.sbuf_tensor("avecsS", [128, 128], F32))
        epsc = es.enter_context(nc.sbuf_tensor("epsc", [128, 1], F32))
        carry_cx = es.enter_context(nc.sbuf_tensor("carry_cx", [128, 16, 16], F32))
        carry_z = es.enter_context(nc.sbuf_tensor("carry_z", [128, 16, 16], F32))
        ps = es.enter_context(nc.psum_tensor("ps", [128, 8, 512], F32))

        G = gy[:, 0:31 * CELL].rearrange("p (c n) -> p c n", n=CELL)
        Y = gy[:, :].rearrange("p (c n) -> p c n", n=512)
        gyf = gy.bitcast(F32)
        STG = [gyf[:, 0:4096], gyf[:, 4096:8192]]

        gy_g_keys = [("G", c) for c in range(31)]
        gy_y_keys = [("Y", c) for c in range(32)]
        gy_s_keys = [("STG", i) for i in range(4)]

        C_S1, C_S2, C_SQ = 0, 1, 2
        C_SGT = [4, 6]
        C_MEAN, C_RSTD = 8, 9
        C_T1 = [10, 12]
        C_T2 = [11, 13]
        C_Z, C_SA, C_SB = 4, 6, 8
        C_DB = [10, 12]
        C_GC, C_CX, C_ACC = 4, 6, 8

        def cells(c0, n):
            return [("scr", c) for c in range(c0, c0 + n)]

        def scr2(c0):
            return scr[:, c0:c0 + 2, :]

        bank_ptr = [0]

        def alloc_banks(nb):
            p = bank_ptr[0]
            p = ((p + nb - 1) // nb) * nb
            if p + nb > 8:
                p = 0
            bank_ptr[0] = p + nb
            return p

        def bkeys(b0, nb):
            return [("ps", b) for b in range(b0, b0 + nb)]

        units = []
        slot_tok = {}

        first_unit_waits = []

        def add_unit(wv, kc0, nk, col0):
            ui = len(units)
            slot = ui % NSLOT
            ne = nk * 256
            ns = (ne + 2047) // 2048
            src = wv[col0 // 256, kc0 * 32768:(kc0 + nk) * 32768].rearrange("(p s l) -> p s l", p=128, s=ns)
            dst = ring[slot][:, 0:nk, :].rearrange("p k c -> p (k c)").rearrange("p (s l) -> p s l", s=ns)
            units.append((src, nk, slot))

            def fn(g, src=src, dst=dst):
                return g.dma_start(out=dst, in_=src)
            ew = list(first_unit_waits) if ui == 0 else []
            P.op("pool", fn, reads=(), writes=[("w", slot)], dma_sem=f"ring{slot}", extra_waits=ew)
            return slot

        def mm_units(wv, col0, kchunks, rhs_fn, rhs_keys_fn, subs, b0, split=False):
            nsub = len(subs)
            nk_tot = len(kchunks)
            thunks = []
            k0 = 0
            while k0 < nk_tot:
                nk = min(16, nk_tot - k0)
                cell = {}
                halves = [(0, nk // 2), (nk // 2, nk)] if (split and nk >= 2) else [(0, nk)]
                for hi, (ka, kb) in enumerate(halves):
                    def thunk(k0=k0, nk=nk, ka=ka, kb=kb, hi=hi, cell=cell):
                        if hi == 0:
                            cell["slot"] = add_unit(wv, kchunks[0] + k0, nk, col0)
                        slot = cell["slot"]

                        def fn(t):
                            inst = None
                            for kk in range(ka, kb):
                                for c in range(2):
                                    for si, (s0, w) in enumerate(subs):
                                        inst = t.matmul(ps[:, b0 + c * nsub + si, 0:w],
                                                        lhsT=ring[slot][:, kk, c * 128:(c + 1) * 128],
                                                        rhs=rhs_fn(k0 + kk, si),
                                                        start=(k0 + kk == 0),
                                                        stop=(k0 + kk == nk_tot - 1))
                            return inst
                        reads = [("w", slot)]
                        for kk in range(ka, kb):
                            reads += rhs_keys_fn(k0 + kk)
                        P.op("pe", fn, reads=reads, writes=bkeys(b0, 2 * nsub))
                    thunks.append(thunk)
                k0 += nk
            return thunks

        def mm_block(wv, col0, kchunks, rhs_fn, rhs_keys_fn, subs, b0):
            for th in mm_units(wv, col0, kchunks, rhs_fn, rhs_keys_fn, subs, b0):
                th()

        def run_interleaved(unit_lists):
            n = max(len(u) for u in unit_lists)
            for i in range(n):
                for u in unit_lists:
                    if i < len(u):
                        u[i]()

        def psv(b0, nsub, w):
            return ps[:, b0:b0 + 2 * nsub, 0:w].rearrange("p (c s) n -> p c s n", s=nsub)

        def colv(t3, ch0, subs):
            s0, w = subs[0]
            n = len(subs)
            return t3[:, ch0:ch0 + 2, s0:s0 + n * w].rearrange("p c (s n) -> p c s n", s=n)

        P.op("sp", lambda q: q.dma_start(out=ident[:], in_=ident_d[:, :]), writes=[("ident",)], dma_sem="sp0")
        P.op("sp", lambda q: q.dma_start(out=vecs[:], in_=vecs_d[:, :]), writes=[("vecs",)], dma_sem="sp1")
        P.op("dve", lambda v: v.memset(ones[:], 1.0), writes=[("ones",)])
        P.op("dve", lambda v: v.memset(epsc[:], EPS), writes=[("epsc",)])
        P.op("dve", lambda v: v.tensor_scalar_mul(out=avecs[:], in0=vecs[:, 0:128], scalar1=ALPHA),
             reads=[("vecs",)], writes=[("avecs",)])

        sp_sem_rr = [0]

        def sp_sem():
            i = sp_sem_rr[0]
            sp_sem_rr[0] = (i + 1) % 8
            return f"sp{i}"

        def stats_accum(pair, c0, T):
            ch = 2 * pair
            sq = scr2(C_SQ)
            P.op("act", lambda a: a.activation(out=sq[:, :, c0:c0 + T], in_=hacc[:, ch:ch + 2, c0:c0 + T],
                                               func=AF.Square),
                 reads=[("hacc", ch), ("hacc", ch + 1)], writes=cells(C_SQ, 2))
            s1 = scr[:, C_S1, c0:c0 + T]
            s2 = scr[:, C_S2, c0:c0 + T]
            if pair == 0:
                P.op("dve", lambda v: v.tensor_tensor(out=s1, in0=hacc[:, ch, c0:c0 + T],
                                                      in1=hacc[:, ch + 1, c0:c0 + T], op=ALU.add),
                     reads=[("hacc", ch), ("hacc", ch + 1)], writes=cells(C_S1, 1))
                P.op("dve", lambda v: v.tensor_tensor(out=s2, in0=sq[:, 0, c0:c0 + T],
                                                      in1=sq[:, 1, c0:c0 + T], op=ALU.add),
                     reads=cells(C_SQ, 2), writes=cells(C_S2, 1))
            else:
                for j in range(2):
                    P.op("dve", lambda v, j=j: v.tensor_tensor(out=s1, in0=s1, in1=hacc[:, ch + j, c0:c0 + T],
                                                               op=ALU.add),
                         reads=[("hacc", ch + j)], writes=cells(C_S1, 1))
                for j in range(2):
                    P.op("dve", lambda v, j=j: v.tensor_tensor(out=s2, in0=s2, in1=sq[:, j, c0:c0 + T],
                                                               op=ALU.add),
                         reads=cells(C_SQ, 2), writes=cells(C_S2, 1))

        def ln_finish(vg, vb, c0, T, final, after_chunk=None):
            subs = [(c0, T // 2), (c0 + T // 2, T // 2)]
            b0 = alloc_banks(4)

            def fn(t):
                inst = None
                for i, cell in enumerate((C_S1, C_S2)):
                    for si, (s0, w) in enumerate(subs):
                        inst = t.matmul(ps[:, b0 + 2 * i + si, 0:w], lhsT=ones[:, :], rhs=scr[:, cell, s0:s0 + w],
                                        start=True, stop=True)
                return inst
            P.op("pe", fn, reads=cells(C_S1, 2) + [("ones",)], writes=bkeys(b0, 4))
            w2 = T // 2
            mean = scr[:, C_MEAN, c0:c0 + T]
            rstd = scr[:, C_RSTD, c0:c0 + T]
            tmp = scr[:, C_T1[0], c0:c0 + T]
            mean2 = mean.rearrange("p (s n) -> p s n", s=2)
            tmp2 = tmp.rearrange("p (s n) -> p s n", s=2)
            P.op("act", lambda a: a.activation(out=mean2, in_=ps[:, b0:b0 + 2, 0:w2], func=AF.Copy, scale=1.0 / D),
                 reads=bkeys(b0, 2), writes=cells(C_MEAN, 1))
            P.op("act", lambda a: a.activation(out=tmp2, in_=ps[:, b0:b0 + 2, 0:w2], func=AF.Square, scale=1.0 / D),
                 reads=bkeys(b0, 2), writes=cells(C_T1[0], 1))
            P.op("dve", lambda v: v.scalar_tensor_tensor(out=tmp2, in0=ps[:, b0 + 2:b0 + 4, 0:w2], scalar=1.0 / D,
                                                         in1=tmp2, op0=ALU.mult, op1=ALU.subtract),
                 reads=bkeys(b0 + 2, 2), writes=cells(C_T1[0], 1))
            P.op("act", lambda a: a.activation(out=tmp, in_=tmp, func=AF.Sqrt, bias=epsc[:, 0:1]),
                 reads=[("epsc",)], writes=cells(C_T1[0], 1))
            P.op("dve", lambda v: v.reciprocal(out=rstd, in_=tmp),
                 reads=cells(C_T1[0], 1), writes=cells(C_RSTD, 1))
            for c in range(NCH):
                t1 = scr[:, C_T1[c % 2], c0:c0 + T]
                t2 = scr[:, C_T2[c % 2], c0:c0 + T]
                hc = hacc[:, c, c0:c0 + T]
                P.op("dve", lambda v, t1=t1, hc=hc: v.tensor_tensor(out=t1, in0=hc, in1=mean, op=ALU.subtract),
                     reads=[("hacc", c)] + cells(C_MEAN, 1), writes=cells(C_T1[c % 2], 1))
                P.op("dve", lambda v, t1=t1, t2=t2: v.tensor_tensor(out=t2, in0=t1, in1=rstd, op=ALU.mult),
                     reads=cells(C_T1[c % 2], 1) + cells(C_RSTD, 1), writes=cells(C_T2[c % 2], 1))
                if final:
                    P.op("act", lambda a, t2=t2, hc=hc, c=c: a.activation(
                        out=hc, in_=t2, func=AF.Identity, bias=vecs[:, vb + c:vb + c + 1],
                        scale=vecs[:, vg + c:vg + c + 1]),
                        reads=cells(C_T2[c % 2], 1) + [("vecs",)], writes=[("hacc", c)])
                else:
                    P.op("act", lambda a, t2=t2, c=c: a.activation(
                        out=hb[:, c, c0:c0 + T], in_=t2, func=AF.Identity, bias=vecs[:, vb + c:vb + c + 1],
                        scale=vecs[:, vg + c:vg + c + 1]),
                        reads=cells(C_T2[c % 2], 1) + [("vecs",)], writes=[("hb", c)])
                    P.op("act", lambda a, t2=t2, hc=hc, c=c: a.activation(
                        out=hc, in_=t2, func=AF.Identity, bias=avecs[:, vb + c:vb + c + 1],
                        scale=avecs[:, vg + c:vg + c + 1]),
                        reads=cells(C_T2[c % 2], 1) + [("avecs",)], writes=[("hacc", c)])
                if after_chunk is not None:
                    after_chunk(c)

        def ffn(li, subs, c0, T):
            wg = kview(w_gu[li])
            wd = kview(w_dn[li])
            nsub = len(subs)
            w = subs[0][1]
            P.alias(gy_y_keys + gy_s_keys + [("SX", i) for i in range(3)] + [("SX", i, "h") for i in range(3)], gy_g_keys)
            for pi, (f0, f1) in enumerate(PARTS):
                npair = (f1 - f0) // 2
                ni = (2 if nsub == 1 else 1) if pi == 0 else 1
                q = 0
                while q < npair:
                    grp = list(range(q, min(q + ni, npair))) if q == 0 else [q]
                    info = []
                    ulists = []
                    for qq in grp:
                        j0 = f0 + 2 * qq
                        bg = alloc_banks(2 * nsub)
                        bu = alloc_banks(2 * nsub)
                        spl = (pi == 0 and q == 0)
                        ulists.append(mm_units(wg, j0 * 128, list(range(NCH)),
                                               lambda k, si: hb[:, k, subs[si][0]:subs[si][0] + w],
                                               lambda k: [("hb", k)], subs, bg, split=spl))
                        ulists.append(mm_units(wg, F + j0 * 128, list(range(NCH)),
                                               lambda k, si: hb[:, k, subs[si][0]:subs[si][0] + w],
                                               lambda k: [("hb", k)], subs, bu, split=spl))
                        info.append((qq, bg, bu))
                    if pi == 0 and q == 0:
                        run_interleaved(ulists)
                    else:
                        for u in ulists:
                            for th in u:
                                th()
                    for (qq, bg, bu) in info:
                        jl = 2 * qq
                        sgt = scr[:, C_SGT[qq % 2]:C_SGT[qq % 2] + 2, :].rearrange("p a n -> p (a n)")[:, 0:2 * nsub * w] \
                            .rearrange("p (c s n) -> p c s n", c=2, s=nsub)
                        P.op("act", lambda a, bg=bg, sgt=sgt: a.activation(out=sgt, in_=psv(bg, nsub, w), func=AF.Silu),
                             reads=bkeys(bg, 2 * nsub), writes=cells(C_SGT[qq % 2], 2))
                        gv = colv(G, jl, subs)
                        P.op("dve", lambda v, bu=bu, sgt=sgt, gv=gv: v.tensor_tensor(out=gv, in0=psv(bu, nsub, w),
                                                                                    in1=sgt, op=ALU.mult),
                             reads=bkeys(bu, 2 * nsub) + cells(C_SGT[qq % 2], 2), writes=[("G", jl), ("G", jl + 1)])
                    q += len(grp)
                last_part = (pi == len(PARTS) - 1)
                for i in range(NCH // 2):
                    bd = alloc_banks(2 * nsub)
                    mm_block(wd, i * 256, list(range(f0, f1)),
                             lambda k, si: G[:, k, subs[si][0]:subs[si][0] + w],
                             lambda k: [("G", k)], subs, bd)
                    hv = colv(hacc, 2 * i, subs)
                    P.op("dve", lambda v, bd=bd, hv=hv: v.scalar_tensor_tensor(
                        out=hv, in0=psv(bd, nsub, w), scalar=0.5, in1=hv, op0=ALU.mult, op1=ALU.add),
                        reads=bkeys(bd, 2 * nsub), writes=[("hacc", 2 * i), ("hacc", 2 * i + 1)])
                    if last_part:
                        stats_accum(i, c0, T)

        def mixer(t, subs1, off):
            wi = kview(w_in)
            wo = kview(w_out)
            pw = kview(pool_w)
            boff = HALO - off
            ns1 = len(subs1)
            w1 = subs1[0][1]
            T1 = ns1 * w1 if ns1 > 1 else w1
            subs2 = [(off, 512)]
            P.alias(gy_g_keys + gy_s_keys + [("SX", i) for i in range(3)], gy_y_keys)

            def hb1(k, si):
                return hb[:, k, subs1[si][0]:subs1[si][0] + w1]

            def hb2(k, si):
                return hb[:, k, off:off + 512]

            def hbk(k):
                return [("hb", k)]

            def buf_view(c0cell):
                b = scr2(c0cell)
                s0 = subs1[0][0] + boff
                return b[:, :, s0:s0 + ns1 * w1].rearrange("p c (s n) -> p c s n", s=ns1)

            def z_units(q, bz, split=False):
                return mm_units(wi, 6144 + 256 * q, list(range(NCH)), hb1, hbk, subs1, bz, split=split)

            def z_block(q, bz=None):
                g = q // 2
                if bz is None:
                    bz = alloc_banks(2 * ns1)
                    for th in z_units(q, bz):
                        th()
                z = scr2(C_Z)
                P.op("act", lambda a: a.activation(out=buf_view(C_Z), in_=psv(bz, ns1, w1), func=AF.Copy),
                     reads=bkeys(bz, 2 * ns1), writes=cells(C_Z, 2))
                if t == 0:
                    P.op("act", lambda a: a.activation(out=carry_z[:, 2 * q:2 * q + 2, :], in_=z[:, :, 512:528],
                                                       func=AF.Copy),
                         reads=cells(C_Z, 2), writes=[("carry_z", q)])
                else:
                    P.op("act", lambda a: a.activation(out=z[:, :, 0:16], in_=carry_z[:, 2 * q:2 * q + 2, :],
                                                       func=AF.Copy),
                         reads=[("carry_z", q)], writes=cells(C_Z, 2))
                cur, curc = z, C_Z
                lo = 0
                bufs = [(scr2(C_SA), C_SA), (scr2(C_SB), C_SB)]
                bi = 0
                for step in (1, 2, 4, 8)[:g + 1]:
                    nxt, nxtc = bufs[bi]
                    bi ^= 1
                    P.op("dve", lambda v, cur=cur, nxt=nxt, lo=lo, step=step: v.tensor_tensor(
                        out=nxt[:, :, lo + step:528], in0=cur[:, :, lo + step:528], in1=cur[:, :, lo:528 - step],
                        op=ALU.add),
                        reads=cells(curc, 2), writes=cells(nxtc, 2))
                    cur, curc = nxt, nxtc
                    lo += step
                W = 2 ** (g + 1)
                dbc = C_DB[g % 2]
                db = scr[:, dbc:dbc + 2, :].rearrange("p a n -> p (a n)").bitcast(BF16)[:, 0:2048] \
                    .rearrange("p (c n) -> p c n", n=512)
                dsl = db[:, 2 * (q % 2):2 * (q % 2) + 2, :]
                P.op("dve", lambda v, cur=cur: v.scalar_tensor_tensor(
                    out=dsl, in0=cur[:, :, 16:528], scalar=1.0 / W, in1=z[:, :, 16:528],
                    op0=ALU.mult, op1=ALU.subtract),
                    reads=cells(curc, 2) + cells(C_Z, 2), writes=[("db", g % 2, q % 2)])

            def p_blocks(g):
                dbc = C_DB[g % 2]
                db = scr[:, dbc:dbc + 2, :].rearrange("p a n -> p (a n)").bitcast(BF16)[:, 0:2048] \
                    .rearrange("p (c n) -> p c n", n=512)
                for ob in range(2):
                    bp = alloc_banks(2)
                    mm_block(pw, 256 * ob, list(range(4 * g, 4 * g + 4)),
                             lambda k, si: db[:, k, :],
                             lambda k: [("db", g % 2, k // 2)], [(0, 512)], bp)
                    for c in range(2):
                        ch = 4 * g + 2 * ob + c
                        P.op("act", lambda a, bp=bp, c=c, ch=ch: a.activation(
                            out=Y[:, 16 + ch, :], in_=ps[:, bp + c, 0:512], func=AF.Identity,
                            scale=vecs[:, V_PS + ch:V_PS + ch + 1]),
                            reads=bkeys(bp + c, 1) + [("vecs",)], writes=[("Y", 16 + ch)])

            def db_guard_begin():
                P.alias(cells(C_DB[0], 4), [("db", a, b) for a in range(2) for b in range(2)])

            def db_guard_end():
                P.alias([("db", a, b) for a in range(2) for b in range(2)], cells(C_DB[0], 4))

            db_guard_begin()
            order = [("z", 0), ("z", 1), ("z", 2), ("z", 3), ("p", 0), ("z", 4), ("z", 5), ("p", 1),
                     ("z", 6), ("z", 7), ("p", 2)]
            nz = 4 if ns1 == 1 else 2
            zb = [alloc_banks(2 * ns1) for _ in range(nz)]
            run_interleaved([z_units(i, zb[i], split=True) for i in range(nz)])
            for kind, i in order:
                if kind == "z":
                    z_block(i, zb[i] if i < nz else None)
                else:
                    p_blocks(i)

            def conv_triple(cp):
                bc = alloc_banks(2 * ns1)
                mm_block(wi, 2048 + 256 * cp, list(range(NCH)), hb1, hbk, subs1, bc)
                P.op("act", lambda a: a.activation(out=buf_view(C_GC), in_=psv(bc, ns1, w1), func=AF.Copy),
                     reads=bkeys(bc, 2 * ns1), writes=cells(C_GC, 2))
                bx = alloc_banks(2 * ns1)
                mm_block(wi, 4096 + 256 * cp, list(range(NCH)), hb1, hbk, subs1, bx)
                cx = scr2(C_CX)
                P.op("dve", lambda v: v.tensor_tensor(out=buf_view(C_CX), in0=psv(bx, ns1, w1), in1=buf_view(C_GC),
                                                      op=ALU.mult),
                     reads=bkeys(bx, 2 * ns1) + cells(C_GC, 2), writes=cells(C_CX, 2))
                if t == 0:
                    P.op("act", lambda a: a.activation(out=carry_cx[:, 2 * cp:2 * cp + 2, :], in_=cx[:, :, 512:528],
                                                       func=AF.Copy),
                         reads=cells(C_CX, 2), writes=[("carry_cx", cp)])
                else:
                    P.op("act", lambda a: a.activation(out=cx[:, :, 0:16], in_=carry_cx[:, 2 * cp:2 * cp + 2, :],
                                                       func=AF.Copy),
                         reads=[("carry_cx", cp)], writes=cells(C_CX, 2))
                bb = alloc_banks(2)
                mm_block(wi, 256 * cp, list(range(NCH)), hb2, hbk, subs2, bb)
                acc = scr[:, C_ACC, 0:512]
                for c in range(2):
                    ch = 2 * cp + c

                    def wcol(kk, ch=ch):
                        return vecs[:, V_CW + 16 * kk + ch:V_CW + 16 * kk + ch + 1]
                    P.op("dve", lambda v, c=c, wcol=wcol: v.tensor_scalar_mul(out=acc, in0=cx[:, c, 14:526],
                                                                              scalar1=wcol(0)),
                         reads=cells(C_CX, 2) + [("vecs",)], writes=cells(C_ACC, 1))
                    P.op("dve", lambda v, c=c, wcol=wcol: v.scalar_tensor_tensor(
                        out=acc, in0=cx[:, c, 15:527], scalar=wcol(1), in1=acc, op0=ALU.mult, op1=ALU.add),
                        reads=cells(C_CX, 2) + [("vecs",)], writes=cells(C_ACC, 1))
                    P.op("dve", lambda v, c=c, wcol=wcol: v.scalar_tensor_tensor(
                        out=acc, in0=cx[:, c, 16:528], scalar=wcol(2), in1=acc, op0=ALU.mult, op1=ALU.add),
                        reads=cells(C_CX, 2) + [("vecs",)], writes=cells(C_ACC, 1))
                    P.op("dve", lambda v, c=c, ch=ch: v.tensor_tensor(out=Y[:, ch, :], in0=ps[:, bb + c, 0:512],
                                                                      in1=acc, op=ALU.mult),
                         reads=bkeys(bb + c, 1) + cells(C_ACC, 1), writes=[("Y", ch)])

            conv_triple(0)
            p_blocks(3)
            db_guard_end()
            for cp in range(1, 8):
                conv_triple(cp)

            for i in range(NCH // 2):
                bo = alloc_banks(2)
                mm_block(wo, i * 256, list(range(NCH)), lambda k, si: Y[:, k, :], lambda k: [("Y", k)],
                         [(0, 512)], bo)
                hv = hacc[:, 2 * i:2 * i + 2, off:off + 512]
                P.op("dve", lambda v, bo=bo, hv=hv: v.tensor_tensor(out=hv, in0=ps[:, bo:bo + 2, 0:512], in1=hv,
                                                                    op=ALU.add),
                     reads=bkeys(bo, 2), writes=[("hacc", 2 * i), ("hacc", 2 * i + 1)])
                stats_accum(i, off, 512)

        gy_sx_keys = [("SX", i) for i in range(3)]
        gy_all_keys = gy_g_keys + gy_y_keys + gy_s_keys + gy_sx_keys

        def slab_load(t, c4, stg, key):
            row0 = 0 if t == 0 else 528
            src = xin[row0:row0 + 512, c4 * 512:(c4 + 1) * 512].rearrange("(tb p) f -> p tb f", p=128)
            tok = P.op("sp", lambda q: q.dma_start(out=stg[:, 0:4, :], in_=src), writes=[key], dma_sem=sp_sem())
            if t == 0 and c4 < 6:
                first_unit_waits.append(tok)
            if t == 0:
                src2 = xin[512:528, c4 * 512:(c4 + 1) * 512]
                P.op("sp", lambda q: q.dma_start(out=stg[0:16, 4, :], in_=src2), writes=[key + ("h",)],
                     dma_sem=sp_sem())

        def slab_consume(t, c4, stg, key, defer=None):
            nblk = 5 if t == 0 else 4
            for tb in range(nblk):
                nt = 128 if tb < 4 else 16
                b = alloc_banks(1)
                rk = key if tb < 4 else key + ("h",)

                def fn(tt, b=b, nt=nt, tb=tb):
                    inst = None
                    for qq in range(4):
                        inst = tt.transpose(out=ps[:, b, qq * 128:qq * 128 + nt],
                                            in_=stg[0:nt, tb, qq * 128:(qq + 1) * 128], identity=ident[0:nt, 0:nt])
                    return inst
                P.op("pe", fn, reads=[rk, ("ident",)], writes=bkeys(b, 1))
                pv = ps[:, b, :].rearrange("p (q n) -> p q n", n=128)[:, :, 0:nt]
                col = tb * 128

                def ev(pv=pv, col=col, nt=nt, b=b):
                    P.op("act", lambda a: a.activation(
                        out=hacc[:, 4 * c4:4 * c4 + 4, col:col + nt], in_=pv, func=AF.Copy, scale=ALPHA),
                        reads=bkeys(b, 1), writes=[("hacc", 4 * c4 + i) for i in range(4)])
                    P.op("dve", lambda v: v.tensor_copy(
                        out=hb[:, 4 * c4:4 * c4 + 4, col:col + nt], in_=pv),
                        reads=[], writes=bkeys(b, 1) + [("hb", 4 * c4 + i) for i in range(4)])
                if defer is None:
                    ev()
                else:
                    defer.append(ev)

        def startup_prologue():
            P.alias(gy_all_keys, gy_sx_keys + [k + ("h",) for k in gy_sx_keys])
            bufs = [gyf[:, i * 2560:(i + 1) * 2560].rearrange("p (tb f) -> p tb f", f=512) for i in range(3)]
            for c4 in range(3):
                slab_load(0, c4, bufs[c4 % 3], ("SX", c4 % 3))
            for c4 in range(8):
                slab_consume(0, c4, bufs[c4 % 3], ("SX", c4 % 3))
                if c4 + 3 < 8:
                    slab_load(0, c4 + 3, bufs[c4 % 3], ("SX", c4 % 3))

        out_toks = []

        def epilogue_slab(t, off, c4, defer=None, all_act=False):
            sb = c4 % 2
            stg = gyf[:, sb * 2048:(sb + 1) * 2048].rearrange("p (tb f) -> p tb f", f=512)
            rows = yout[t * 512:(t + 1) * 512, :].rearrange("(tb p) f -> p tb f", p=128)
            for tb in range(4):
                col = off + tb * 128
                b = alloc_banks(1)

                def fn(tt, b=b, col=col):
                    inst = None
                    for qq in range(4):
                        c = 4 * c4 + qq
                        inst = tt.transpose(out=ps[:, b, qq * 128:(qq + 1) * 128],
                                            in_=hacc[:, c, col:col + 128], identity=ident[:, :])
                    return inst
                P.op("pe", fn, reads=[("hacc", 4 * c4 + i) for i in range(4)] + [("ident",)], writes=bkeys(b, 1))
                dst = stg[:, tb, :]

                def ev(b=b, dst=dst, tb=tb):
                    if all_act or tb % 2 == 0:
                        P.op("act", lambda a: a.activation(out=dst, in_=ps[:, b, :], func=AF.Copy),
                             reads=bkeys(b, 1), writes=[("STG", sb)])
                    else:
                        P.op("dve", lambda v: v.tensor_copy(out=dst, in_=ps[:, b, :]),
                             reads=bkeys(b, 1), writes=[("STG", sb)])
                if defer is None:
                    ev()
                else:
                    defer.append(ev)

            def store():
                tok = P.op("sp", lambda q: q.dma_start(out=rows[:, :, c4 * 512:(c4 + 1) * 512], in_=stg),
                           reads=[("STG", sb)], dma_sem=sp_sem())
                out_toks.append(tok)
            if defer is None:
                store()
            else:
                defer.append(store)

        def boundary(t, off, nxt):
            P.alias(gy_all_keys, gy_s_keys)
            xb = [gyf[:, (2 + i) * 2048:(3 + i) * 2048].rearrange("p (tb f) -> p tb f", f=512) for i in range(2)]
            if nxt is not None:
                for c4 in range(2):
                    slab_load(nxt, c4, xb[c4 % 2], ("STG", 2 + c4 % 2))

            pending = []

            def cb(c):
                if c % 4 != 3:
                    return
                c4 = c // 4
                prev = list(pending)
                del pending[:]
                for ev in prev:
                    ev()
                epilogue_slab(t, off, c4, defer=pending, all_act=(nxt is None))
                if nxt is not None and c4 >= 1:
                    slab_consume(nxt, c4 - 1, xb[(c4 - 1) % 2], ("STG", 2 + (c4 - 1) % 2), defer=pending)
                    if c4 + 1 < 8:
                        slab_load(nxt, c4 + 1, xb[(c4 + 1) % 2], ("STG", 2 + (c4 + 1) % 2))
            ln_finish(V_LN + 128, V_LN + 160, off, 512, final=True, after_chunk=cb)
            for ev in pending:
                ev()
            if nxt is not None:
                slab_consume(nxt, 7, xb[1], ("STG", 3))

        startup_prologue()
        for t in range(2):
            if t == 0:
                subs1 = [(0, 264), (264, 264)]
                off, T1 = 16, 528
            else:
                subs1 = [(0, 512)]
                off, T1 = 0, 512
            subs2 = [(off, 512)]
            ffn(0, subs1, 0, T1)
            ln_finish(V_LN + 0, V_LN + 32, 0, T1, final=False)
            mixer(t, subs1, off)
            ln_finish(V_LN + 64, V_LN + 96, off, 512, final=False)
            ffn(1, subs2, off, 512)
            boundary(t, off, 1 if t == 0 else None)

        P.op("sp", lambda q: q.nop(), extra_waits=out_toks + P.all_tokens([("STG", i) for i in range(4)]))

        P.finalize()

        sems = {n: es.enter_context(nc.semaphore(f"sem_{n}")) for n in P.engs}
        dma_sems = {n: es.enter_context(nc.semaphore(f"dsem_{n}")) for n in P.dma_sem_count}
        block = es.enter_context(nc.Block())

        @block.tensor
        def _(h):
            P.emit("pe", h, sems, dma_sems)

        @block.scalar
        def _(h):
            P.emit("act", h, sems, dma_sems)

        @block.vector
        def _(h):
            P.emit("dve", h, sems, dma_sems)

        @block.gpsimd
        def _(h):
            P.emit("pool", h, sems, dma_sems)

        @block.sync
        def _(h):
            P.emit("sp", h, sems, dma_sems)

    return nc


_NC_CACHE = {}


def kernel(x, meta_tokens, ffn1_w_gu, ffn1_w_down, ln1_g, ln1_b, w_in, conv_w, pool_w, pool_scale, w_out,
           ln2_g, ln2_b, ffn2_w_gu, ffn2_w_down, ln3_g, ln3_b):
    f32 = np.float32
    x = np.asarray(x, dtype=f32)
    meta = np.asarray(meta_tokens, dtype=f32)
    B = x.shape[0]

    def col(v):
        v = np.asarray(v, dtype=f32).reshape(-1)
        return v.reshape(-1, 128).T

    vecs = np.zeros((128, NV), dtype=f32)
    for i, v in enumerate([ln1_g, ln1_b, ln2_g, ln2_b, ln3_g, ln3_b]):
        vecs[:, V_LN + 32 * i:V_LN + 32 * (i + 1)] = col(v)
    cw = np.asarray(conv_w, dtype=f32).reshape(3, 2048)
    for k in range(3):
        vecs[:, V_CW + 16 * k:V_CW + 16 * (k + 1)] = col(cw[k])
    vecs[:, V_PS:V_PS + 16] = col(pool_scale)
    ident = np.eye(128, dtype=f32)

    def tile_w(w, K, N, units):
        w4 = np.asarray(w, dtype=f32).reshape(K // 128, 128, N // 256, 256)
        out = np.empty((N // 256, K * 256), dtype=f32)
        for (k0, nk) in units:
            blk = w4[k0:k0 + nk].transpose(2, 1, 0, 3)
            out[:, k0 * 32768:(k0 + nk) * 32768] = blk.reshape(N // 256, 128 * nk * 256)
        return out

    u32 = [(0, 16), (16, 16)]
    udn = []
    for (f0, f1) in PARTS:
        udn += [(f0, 16), (f0 + 16, f1 - f0 - 16)]
    upl = [(0, 4), (4, 4), (8, 4), (12, 4)]
    shared = {
        "ident": ident,
        "vecs": vecs,
        "w_gu1": tile_w(ffn1_w_gu, D, 2 * F, u32),
        "w_gu2": tile_w(ffn2_w_gu, D, 2 * F, u32),
        "w_d1": tile_w(ffn1_w_down, F, D, udn),
        "w_d2": tile_w(ffn2_w_down, F, D, udn),
        "w_in": tile_w(w_in, D, 8192, u32),
        "pool_w": tile_w(pool_w, 2048, 512, upl),
        "w_out": tile_w(w_out, D, D, u32),
    }
    in_maps = []
    for core in range(NCORES):
        b, half = core // 2, core % 2
        halo = meta if half == 0 else x[b, TOK - HALO:TOK]
        xin = np.concatenate([halo, x[b, half * TOK:(half + 1) * TOK]], axis=0)
        m = dict(shared)
        m["xin"] = np.ascontiguousarray(xin)
        in_maps.append(m)

    if "nc" not in _NC_CACHE:
        _NC_CACHE["nc"] = build_nc()
    nc = _NC_CACHE["nc"]
    res = run_bass_kernel_spmd(nc, in_maps, core_ids=list(range(NCORES)))
    out = np.empty((B, 2 * TOK, D), dtype=f32)
    for core in range(NCORES):
        b, half = core // 2, core % 2
        out[b, half * TOK:(half + 1) * TOK] = res.results[core]["y"]
    return out
```
